# Optimizing a Trainium2 kernel written in Bass

```python
import jax
import jax.numpy as jnp
from jax import lax
import numpy as np

D_MODEL = 1024
BATCH = 4
SEQ = 8192
DEPTH = 4

N_MEM = 256
X_HEADS = 4
X_HEAD_DIM = 128
X_WIDTH = X_HEADS * X_HEAD_DIM
MLA_HEADS = 12
QK_NOPE = 128
QK_ROPE = 64
V_HEAD = 128
Q_LORA = 384
KV_LORA = 256
ROPE_THETA = 10000.0
MIX_WIDTH = MLA_HEADS * V_HEAD
INNER = MIX_WIDTH + X_WIDTH
RWKV_HEAD = 64
RWKV_HEADS = MIX_WIDTH // RWKV_HEAD
DECAY_LORA = 64
ICLR_LORA = 64
GN_EPS = 64e-5
NORM_EPS = 1e-6
Q_BLOCK = 128
N_MIXERS = 2
N_MLA = (DEPTH + 1) // 2
N_RWKV = DEPTH // 2
MLA_IN = Q_LORA + KV_LORA + QK_ROPE + X_WIDTH + INNER
RWKV_SHIFT = 3 * MIX_WIDTH + DECAY_LORA + ICLR_LORA
RWKV_IN = RWKV_SHIFT + X_WIDTH + INNER

kernel_name = 'hybrid_mla_rwkv7_memxattn_gated'


def split_cols(u, sizes):
    out, start = [], 0
    for n in sizes:
        out.append(u[..., start:start + n])
        start += n
    return out


def rmsnorm(x, g):
    xf = x.astype(jnp.float32)
    y = xf * lax.rsqrt(jnp.mean(xf * xf, axis=-1, keepdims=True) + NORM_EPS)
    return (y * g.astype(jnp.float32)).astype(x.dtype)


def rope_tables(positions):
    inv_freq = ROPE_THETA ** (-jnp.arange(0, QK_ROPE, 2, dtype=jnp.float32) / QK_ROPE)
    ang = positions.astype(jnp.float32)[..., None] * inv_freq
    return jnp.cos(ang), jnp.sin(ang)


def apply_rope(x, cos, sin):
    cos = cos.astype(x.dtype)
    sin = sin.astype(x.dtype)
    x1, x2 = x[..., 0::2], x[..., 1::2]
    return jnp.stack([x1 * cos - x2 * sin, x1 * sin + x2 * cos], axis=-1).reshape(x.shape)


def causal_block_attention(q_nope, q_rope, k_nope, k_rope, v):
    B, S, H, _ = q_nope.shape
    n_blocks = S // Q_BLOCK
    scale = (QK_NOPE + QK_ROPE) ** -0.5
    key_idx = jnp.arange(S)

    def to_blocks(t):
        return jnp.moveaxis(t.reshape(B, n_blocks, Q_BLOCK, *t.shape[2:]), 1, 0)

    def one_block(args):
        qn, qr, blk = args
        s = (jnp.einsum('bqhd,bkhd->bhqk', qn, k_nope)
             + jnp.einsum('bqhr,bkr->bhqk', qr, k_rope)).astype(jnp.float32) * scale
        q_idx = blk * Q_BLOCK + jnp.arange(Q_BLOCK)
        s = jnp.where(key_idx[None, :] <= q_idx[:, None], s, -jnp.inf)
        p = jax.nn.softmax(s, axis=-1).astype(v.dtype)
        return jnp.einsum('bhqk,bkhd->bqhd', p, v)

    out = lax.map(one_block, (to_blocks(q_nope), to_blocks(q_rope), jnp.arange(n_blocks)))
    return jnp.moveaxis(out, 0, 1).reshape(B, S, H * V_HEAD)


def mla_mixer(c_q, c_kv, k_rope_raw, cos, sin, q_norm_g, kv_norm_g, w_uq, w_ukv):
    B, S, _ = c_q.shape
    q = (rmsnorm(c_q, q_norm_g) @ w_uq).reshape(B, S, MLA_HEADS, QK_NOPE + QK_ROPE)
    q_nope = q[..., :QK_NOPE]
    q_rope = apply_rope(q[..., QK_NOPE:], cos[:, :, None, :], sin[:, :, None, :])
    kv = (rmsnorm(c_kv, kv_norm_g) @ w_ukv).reshape(B, S, MLA_HEADS, QK_NOPE + V_HEAD)
    k_nope, v = kv[..., :QK_NOPE], kv[..., QK_NOPE:]
    k_rope = apply_rope(k_rope_raw, cos, sin)
    return causal_block_attention(q_nope, q_rope, k_nope, k_rope, v)


def rwkv7_scan(r, w, k, v, kk, a):
    B, S, H, N = r.shape

    def step(state, inp):
        r_t, w_t, k_t, v_t, kk_t, a_t = inp
        sa = jnp.einsum('bhvk,bhk->bhv', state, -kk_t)
        state = (state * w_t[:, :, None, :]
                 + sa[..., None] * (kk_t * a_t)[:, :, None, :]
                 + v_t[..., None] * k_t[:, :, None, :])
        return state, jnp.einsum('bhvk,bhk->bhv', state, r_t)

    xs = tuple(jnp.moveaxis(t, 1, 0) for t in (r, w, k, v, kk, a))
    _, y = lax.scan(step, jnp.zeros((B, H, N, N), jnp.float32), xs)
    return jnp.moveaxis(y, 0, 1)


def rwkv7_mixer(u_shift, mu, w0, w2, a0, a2, k_k, k_a, r_k, gn_w, gn_b):
    B, S, _ = u_shift.shape
    f32 = jnp.float32
    u = u_shift.astype(f32)
    u_prev = jnp.pad(u[:, :-1], ((0, 0), (1, 0), (0, 0)))
    u = u + (u_prev - u) * mu.astype(f32)
    r, k, v, wd, ad = split_cols(u, (MIX_WIDTH, MIX_WIDTH, MIX_WIDTH, DECAY_LORA, ICLR_LORA))
    w = -jax.nn.softplus(-(w0.astype(f32) + jnp.tanh(wd) @ w2.astype(f32))) - 0.5
    decay = jnp.exp(-jnp.exp(w))
    a = jax.nn.sigmoid(a0.astype(f32) + ad @ a2.astype(f32))

    def heads(t):
        return t.reshape(B, S, RWKV_HEADS, RWKV_HEAD)

    kk = heads(k * k_k.astype(f32))
    kk = kk / jnp.maximum(jnp.linalg.norm(kk, axis=-1, keepdims=True), 1e-12)
    k = k * (1.0 + (a - 1.0) * k_a.astype(f32))
    r, decay, k, v, a = heads(r), heads(decay), heads(k), heads(v), heads(a)
    y = rwkv7_scan(r, decay, k, v, kk, a)
    mean = jnp.mean(y, axis=-1, keepdims=True)
    var = jnp.mean(jnp.square(y - mean), axis=-1, keepdims=True)
    y = (y - mean) * lax.rsqrt(var + GN_EPS)
    y = y * gn_w.astype(f32).reshape(RWKV_HEADS, RWKV_HEAD) + gn_b.astype(f32).reshape(RWKV_HEADS, RWKV_HEAD)
    bonus = jnp.sum(r * k * r_k.astype(f32).reshape(RWKV_HEADS, RWKV_HEAD), axis=-1, keepdims=True) * v
    return (y + bonus).reshape(B, S, MIX_WIDTH).astype(u_shift.dtype)


def memory_attention(q, mem_k, mem_v):
    B, S = q.shape[:2]
    s = jnp.einsum('bshd,bmhd->bhsm', q, mem_k).astype(jnp.float32) * X_HEAD_DIM ** -0.5
    p = jax.nn.softmax(s, axis=-1).astype(mem_v.dtype)
    return jnp.einsum('bhsm,bmhd->bshd', p, mem_v).reshape(B, S, X_WIDTH)


def setup_inputs(seed: int = 0) -> dict:
    key = jax.random.key(seed)
    ks = iter(jax.random.split(key, 32))

    def nrm(shape, scale):
        return scale * jax.random.normal(next(ks), shape, jnp.float32)

    def gain(shape):
        return 1.0 + nrm(shape, 0.02)

    def unif(shape, lo, hi):
        return jax.random.uniform(next(ks), shape, jnp.float32, lo, hi)

    x = nrm((BATCH, SEQ, D_MODEL), 1.0)
    mem = nrm((BATCH, N_MEM, D_MODEL), 1.0)
    offset = jax.random.randint(next(ks), (BATCH, 1), 0, 4096, dtype=jnp.int32)
    positions = (offset + jnp.arange(SEQ, dtype=jnp.int32)[None, :]).astype(jnp.int32)
    return {
        'x': x,
        'mem': mem,
        'positions': positions,
        'norm_g': gain((DEPTH, D_MODEL)),
        'mem_norm_g': gain((DEPTH, D_MODEL)),
        'w_mem_kv': nrm((DEPTH, D_MODEL, 2 * X_WIDTH), D_MODEL ** -0.5),
        'w_in_mla': nrm((N_MLA, D_MODEL, MLA_IN), D_MODEL ** -0.5),
        'mla_q_norm_g': gain((N_MLA, Q_LORA)),
        'mla_kv_norm_g': gain((N_MLA, KV_LORA)),
        'mla_w_uq': nrm((N_MLA, Q_LORA, MLA_HEADS * (QK_NOPE + QK_ROPE)), Q_LORA ** -0.5),
        'mla_w_ukv': nrm((N_MLA, KV_LORA, MLA_HEADS * (QK_NOPE + V_HEAD)), KV_LORA ** -0.5),
        'w_in_rwkv': nrm((N_RWKV, D_MODEL, RWKV_IN), D_MODEL ** -0.5),
        'rwkv_mu': unif((N_RWKV, RWKV_SHIFT), 0.0, 1.0),
        'rwkv_w0': unif((N_RWKV, MIX_WIDTH), -6.0, -1.0),
        'rwkv_w2': nrm((N_RWKV, DECAY_LORA, MIX_WIDTH), 0.1 * DECAY_LORA ** -0.5),
        'rwkv_a0': nrm((N_RWKV, MIX_WIDTH), 0.1),
        'rwkv_a2': nrm((N_RWKV, ICLR_LORA, MIX_WIDTH), 0.1 * ICLR_LORA ** -0.5),
        'rwkv_k_k': 0.85 + nrm((N_RWKV, MIX_WIDTH), 0.02),
        'rwkv_k_a': gain((N_RWKV, MIX_WIDTH)),
        'rwkv_r_k': nrm((N_RWKV, MIX_WIDTH), 0.1),
        'rwkv_gn_w': gain((N_RWKV, MIX_WIDTH)),
        'rwkv_gn_b': nrm((N_RWKV, MIX_WIDTH), 0.02),
        'w_out': nrm((DEPTH, INNER, D_MODEL), INNER ** -0.5),
        'final_g': gain((D_MODEL,)),
    }


def reference(x, mem, positions, norm_g, mem_norm_g, w_mem_kv, w_in_mla, mla_q_norm_g,
              mla_kv_norm_g, mla_w_uq, mla_w_ukv, w_in_rwkv, rwkv_mu, rwkv_w0, rwkv_w2,
              rwkv_a0, rwkv_a2, rwkv_k_k, rwkv_k_a, rwkv_r_k, rwkv_gn_w, rwkv_gn_b,
              w_out, final_g):
    B, S, _ = x.shape
    M = mem.shape[1]
    cos, sin = rope_tables(positions)
    for i in range(DEPTH):
        h = rmsnorm(x, norm_g[i])
        m = rmsnorm(mem, mem_norm_g[i])
        mem_k, mem_v = split_cols(m @ w_mem_kv[i], (X_WIDTH, X_WIDTH))
        mem_k = mem_k.reshape(B, M, X_HEADS, X_HEAD_DIM)
        mem_v = mem_v.reshape(B, M, X_HEADS, X_HEAD_DIM)
        j = i // N_MIXERS
        if i % N_MIXERS == 0:
            u = h @ w_in_mla[j]
            c_q, c_kv, k_rope_raw, q_mem, gate = split_cols(
                u, (Q_LORA, KV_LORA, QK_ROPE, X_WIDTH, INNER))
            mix = mla_mixer(c_q, c_kv, k_rope_raw, cos, sin, mla_q_norm_g[j],
                            mla_kv_norm_g[j], mla_w_uq[j], mla_w_ukv[j])
        else:
            u = h @ w_in_rwkv[j]
            u_shift, q_mem, gate = split_cols(u, (RWKV_SHIFT, X_WIDTH, INNER))
            mix = rwkv7_mixer(u_shift, rwkv_mu[j], rwkv_w0[j], rwkv_w2[j], rwkv_a0[j],
                              rwkv_a2[j], rwkv_k_k[j], rwkv_k_a[j], rwkv_r_k[j],
                              rwkv_gn_w[j], rwkv_gn_b[j])
        mem_out = memory_attention(q_mem.reshape(B, S, X_HEADS, X_HEAD_DIM), mem_k, mem_v)
        y = jnp.concatenate([mix, mem_out], axis=-1) * jax.nn.silu(gate)
        x = x + y @ w_out[i]
    return rmsnorm(x, final_g)
```

```python
import numpy as np
from contextlib import ExitStack
import concourse.bass as bass
import concourse.mybir as mybir
from concourse.bass_utils import run_bass_kernel_spmd

F32 = mybir.dt.float32
BF16 = mybir.dt.bfloat16
I32 = mybir.dt.int32
AF = mybir.ActivationFunctionType
ALU = mybir.AluOpType
AX = mybir.AxisListType


class Sched:
    ENG = ('pe', 'act', 'dve', 'pool', 'sp')

    def __init__(self, nc, stack, direct=True):
        self.direct = direct
        self.engobj = {'pe': nc.tensor, 'act': nc.scalar, 'dve': nc.vector, 'pool': nc.gpsimd, 'sp': nc.sync}
        self.nc = nc
        self.stack = stack
        self.ops = {e: [] for e in self.ENG}
        self.sems = {}
        self.cnt = {}
        self.waited = {}
        self.lastw = {}
        self.readers = {}
        for e in ('pe', 'act', 'dve', 'pool'):
            self._sem('E_' + e)
        self.nops = 0

    def _sem(self, key):
        if key not in self.sems:
            self.sems[key] = self.stack.enter_context(self.nc.semaphore(key))
            self.cnt[key] = 0
        return self.sems[key]

    def _wait(self, eng, key, val):
        if val <= 0:
            return
        if self.waited.get((eng, key), 0) >= val:
            return
        self.waited[(eng, key)] = val
        sem = self.sems[key]
        self._emit(eng, lambda e, sem=sem, val=val: e.wait_ge(sem, val))

    def _emit(self, eng, f):
        if self.direct:
            f(self.engobj[eng])
        else:
            self.ops[eng].append(f)

    def _deps(self, eng, reads, writes):
        deps = []
        for r in reads:
            if r in self.lastw:
                deps.append(self.lastw[r])
        for w in writes:
            if w in self.lastw:
                deps.append(self.lastw[w])
            for k, v in self.readers.get(w, {}).items():
                deps.append((k, v))
        for k, v in deps:
            if eng == 'pe' and k == 'E_pe':
                continue
            self._wait(eng, k, v)

    def _commit(self, key, val, reads, writes):
        for r in reads:
            d = self.readers.setdefault(r, {})
            d[key] = max(d.get(key, 0), val)
        for w in writes:
            self.lastw[w] = (key, val)
            self.readers[w] = {}

    def op(self, eng, fn, reads=(), writes=()):
        self._deps(eng, reads, writes)
        key = 'E_' + eng
        self.cnt[key] += 1
        val = self.cnt[key]
        sem = self.sems[key]
        self._emit(eng, lambda e, fn=fn, sem=sem: fn(e).then_inc(sem, 1))
        self._commit(key, val, reads, writes)
        self.nops += 1

    def dma(self, out, in_, slot, reads=(), writes=(), q='sp'):
        key = 'D_' + slot
        self._sem(key)
        self._deps(q, reads, writes)
        self.cnt[key] += 16
        val = self.cnt[key]
        sem = self.sems[key]
        self._emit(q, lambda e, out=out, in_=in_, sem=sem: e.dma_start(out=out, in_=in_).then_inc(sem, 16))
        self._commit(key, val, reads, writes)
        self.nops += 1

    def finish(self):
        for key, c in self.cnt.items():
            self._wait('sp', key, c)
        if self.direct:
            return
        nc = self.nc
        ops = self.ops
        with nc.Block() as block:
            @block.tensor
            def _(e):
                for f in ops['pe']:
                    f(e)

            @block.scalar
            def _(e):
                for f in ops['act']:
                    f(e)

            @block.vector
            def _(e):
                for f in ops['dve']:
                    f(e)

            @block.gpsimd
            def _(e):
                for f in ops['pool']:
                    f(e)

            @block.sync
            def _(e):
                for f in ops['sp']:
                    f(e)


def _barrier(self):
    for eng in ('pe', 'act', 'dve', 'pool', 'sp'):
        for key, c in self.cnt.items():
            self._wait(eng, key, c)
Sched.barrier = _barrier


class K:
    def __init__(self, nc, st):
        self.nc, self.st = nc, st
        self.S = Sched(nc, st, direct=True)
        self.uid = 0

    def sb(self, name, shape, dt, st=None):
        return (st or self.st).enter_context(self.nc.sbuf_tensor(name, list(shape), dt))

    def ps(self, name, shape, dt, st=None):
        return (st or self.st).enter_context(self.nc.psum_tensor(name, list(shape), dt))

    def mm(self, out, lhsT, rhs, start, stop, reads, writes):
        self.S.op('pe', lambda e: e.matmul(out, lhsT, rhs, start=start, stop=stop), reads, writes)

    def act(self, out, in_, func, reads, writes, **kw):
        self.S.op('act', lambda e: e.activation(out, in_, func, **kw), reads, writes)

    def tt(self, eng, out, a, b, op, reads, writes):
        self.S.op(eng, lambda e: e.tensor_tensor(out, a, b, op), reads, writes)

    def ts(self, eng, out, a, s1, s2, op0, op1, reads, writes):
        if s2 is None:
            self.S.op(eng, lambda e: e.tensor_scalar(out, a, s1, None, op0), reads, writes)
        else:
            self.S.op(eng, lambda e: e.tensor_scalar(out, a, s1, s2, op0, op1), reads, writes)

    def stt(self, eng, out, a, s, b, op0, op1, reads, writes):
        self.S.op(eng, lambda e: e.scalar_tensor_tensor(out, a, s, b, op0, op1), reads, writes)

    def cp(self, eng, out, in_, reads, writes):
        if eng == 'act':
            self.S.op('act', lambda e: e.copy(out, in_), reads, writes)
        else:
            self.S.op(eng, lambda e: e.tensor_copy(out, in_), reads, writes)


PI = float(np.pi)
TN = 512


def load_w(k, dst, src, C, N, stg, eng_cycle=('act', 'dve', 'pool')):
    i = k.uid
    for c in range(C):
        for n0 in range(0, N, 2048):
            n1 = min(N, n0 + 2048)
            s = i % 2
            k.S.dma(stg[s][:, 0:n1 - n0], src[c * 128:(c + 1) * 128, n0:n1], 'stg%d' % s, writes=['stg%d' % s], q='sp')
            k.cp(eng_cycle[i % len(eng_cycle)], dst[:, c, n0:n1], stg[s][:, 0:n1 - n0], ['stg%d' % s], [('w', id(dst), c, n0)])
            i += 1
    k.uid = i


def wkeys(dst, C, N):
    return [('w', id(dst), c, n0) for c in range(C) for n0 in range(0, N, 2048)]


def rmsnorm_fm(k, srcs, src_keys, g, D, N, ones, statbank, sqr, tmpf, outs, out_keys, eps=1e-6):
    C = len(srcs)
    for c in range(C):
        s = k.uid % 2
        k.uid += 1
        k.act(sqr[s][:, 0:N], srcs[c], AF.Square, [src_keys[c]], ['sqr%d' % s])
        k.mm(statbank[:, 0:N], ones[:], sqr[s][:, 0:N], c == 0, c == C - 1, ['sqr%d' % s, 'ones'], ['statbank'])
    k.act(tmpf[:, 0:N], statbank[:, 0:N], AF.Ln, ['statbank'], ['tmpf'], scale=1.0 / D, bias=k.epsb[:])
    k.act(tmpf[:, 0:N], tmpf[:, 0:N], AF.Exp, ['tmpf'], ['tmpf'], scale=-0.5)
    for c in range(C):
        k.stt('dve', outs[c], srcs[c], g[:, c:c + 1], tmpf[:, 0:N], ALU.mult, ALU.mult,
              [src_keys[c], 'tmpf', 'gains'], [out_keys[c]])


def mla_layer(k, nc, SQ, has_partials):
    S = k.S
    NT = SQ // TN
    NB = SQ // 128
    di = lambda n, sh, dt=F32: nc.dram_tensor(n, list(sh), dt, kind="ExternalInput").ap()
    do = lambda n, sh, dt=F32: nc.dram_tensor(n, list(sh), dt, kind="ExternalOutput").ap()
    dsc = lambda n, sh, dt=BF16: nc.dram_tensor(n, list(sh), dt, kind="Internal").ap()
    xT = di("xT", [1024, SQ])
    if has_partials:
        p0T = di("p0T", [1024, SQ]); p1T = di("p1T", [1024, SQ])
        xnT = do("xnT", [1024, SQ])
    gains_d = di("gains", [128, 21])
    w_in_d = di("w_in", [1024, 2048]); w_uq_d = di("w_uq", [384, 1536]); w_ukv_d = di("w_ukv", [256, 1536])
    w_out_d = di("w_out", [1024, 1024]); w_mkv_d = di("w_mkv", [1024, 512]); memT_d = di("memT", [1024, 256])
    pos_d = nc.dram_tensor("pos", [SQ], I32, kind="ExternalInput").ap()
    ropec_d = di("ropec", [128, 4])
    pT = do("pT", [1024, SQ])
    cqnT = dsc("cqnT", [128, 3, SQ]); ckvnT = dsc("ckvnT", [128, 2, SQ]); gateT = dsc("gateT", [128, 6, SQ]); yT = dsc("yT", [128, 8, SQ])
    xv = xT.rearrange("(c p) t -> p c t", p=128)
    if has_partials:
        p0v = p0T.rearrange("(c p) t -> p c t", p=128); p1v = p1T.rearrange("(c p) t -> p c t", p=128)
        xnv = xnT.rearrange("(c p) t -> p c t", p=128)
    pv = pT.rearrange("(c p) t -> p c t", p=128)
    memv = memT_d.rearrange("(c p) t -> p c t", p=128)

    banks = [k.ps("bank%d" % i, [128, 512], F32) for i in range(8)]
    ones = k.sb("ones", [128, 128], BF16)
    mask = k.sb("mask", [128, 128], BF16)
    maskf = k.sb("maskf", [128, 128], F32)
    gains = k.sb("gainsb", [128, 21], F32)
    ropec = k.sb("ropecb", [128, 4], F32)
    k.epsb = k.sb("epsb", [128, 1], F32)
    TAB = k.sb("TAB", [128, SQ], BF16)
    KrT = k.sb("KrT", [64, SQ], BF16)
    stg = [k.sb("stg%d" % i, [128, 2048], F32) for i in range(2)]
    sqr = [k.sb("sqr%d" % i, [128, 512], BF16) for i in range(2)]
    tmpf = k.sb("tmpf", [128, 512], F32)
    S.op('pool', lambda e: e.memset(ones[:], 1.0), writes=['ones'])
    S.op('pool', lambda e: e.memset(k.epsb[:], 1e-6), writes=['epsb'])
    S.op('pool', lambda e: e.memset(maskf[:], 1.0), writes=['maskf'])
    S.op('pool', lambda e: e.affine_select(out=maskf[:], in_=maskf[:], pattern=[[1, 128]], compare_op=ALU.is_ge, fill=0.0, base=0, channel_multiplier=-1), ['maskf'], ['maskf'])
    k.cp('dve', mask[:], maskf[:], ['maskf'], ['mask'])
    S.dma(gains[:], gains_d, 'c_gains', writes=['gains'])
    S.dma(ropec[:], ropec_d, 'c_ropec', writes=['ropec'])
    gx, gq, gkv, gm = gains[:, 0:8], gains[:, 8:11], gains[:, 11:13], gains[:, 13:21]

    with ExitStack() as ph:
        posi = k.sb("posi", [128, 2048], I32, ph)
        posf = k.sb("posf", [128, 2048], F32, ph)
        kf = k.sb("kf", [128, 2048], F32, ph)
        C1 = 6.28125
        C2 = 2 * PI - C1
        for c0 in range(0, SQ, 2048):
            n = min(2048, SQ - c0)
            S.dma(posi[:, 0:n], pos_d[c0:c0 + n].partition_broadcast(128), 'posi', writes=['posi'])
            k.cp('dve', posf[:, 0:n], posi[:, 0:n], ['posi'], ['posf'])
            k.ts('dve', posf[:, 0:n], posf[:, 0:n], ropec[:, 0:1], ropec[:, 1:2], ALU.mult, ALU.add, ['posf', 'ropec'], ['posf'])
            k.ts('dve', kf[:, 0:n], posf[:, 0:n], 1.0 / (2 * PI), None, ALU.mult, None, ['posf'], ['kf'])
            k.cp('dve', posi[:, 0:n], kf[:, 0:n], ['kf'], ['posi'])
            k.cp('dve', kf[:, 0:n], posi[:, 0:n], ['posi'], ['kf'])
            k.stt('dve', posf[:, 0:n], kf[:, 0:n], -C1, posf[:, 0:n], ALU.mult, ALU.add, ['kf', 'posf'], ['posf'])
            k.stt('dve', posf[:, 0:n], kf[:, 0:n], -C2, posf[:, 0:n], ALU.mult, ALU.add, ['kf', 'posf'], ['posf'])
            k.ts('dve', kf[:, 0:n], posf[:, 0:n], PI, -2 * PI, ALU.is_gt, ALU.mult, ['posf'], ['kf'])
            k.tt('dve', posf[:, 0:n], posf[:, 0:n], kf[:, 0:n], ALU.add, ['posf', 'kf'], ['posf'])
            k.ts('dve', posf[:, 0:n], posf[:, 0:n], PI, -PI, ALU.min, ALU.max, ['posf'], ['posf'])
            k.act(TAB[:, c0:c0 + n], posf[:, 0:n], AF.Sin, ['posf', 'ropec'], [('TAB', c0 // 2048)], scale=ropec[:, 2:3])
        S.barrier()
    tabkey = lambda col: ('TAB', col // 2048)

    with ExitStack() as ph:
        w_in = k.sb("w_in_b", [128, 8, 2048], BF16, ph)
        w_mkv = k.sb("w_mkv_b", [128, 8, 512], BF16, ph)
        load_w(k, w_mkv, w_mkv_d, 8, 512, stg)
        load_w(k, w_in, w_in_d, 8, 2048, stg)
        kw_in = wkeys(w_in, 8, 2048); kw_mkv = wkeys(w_mkv, 8, 512)
        memx = k.sb("memx", [128, 8, 256], F32, ph)
        memn = k.sb("memn", [128, 8, 256], BF16, ph)
        mKT = k.sb("mKT", [128, 2, 256], BF16, ph)
        mV = k.sb("mV", [128, 2, 2, 128], BF16, ph)
        S.dma(memx[:], memv, 'memx', writes=['memx'])
        rmsnorm_fm(k, [memx[:, c, :] for c in range(8)], ['memx'] * 8, gm, 1024, 256, ones, banks[3], sqr, tmpf,
                   [memn[:, c, :] for c in range(8)], [('memn', c) for c in range(8)])
        memn_keys = [('memn', c) for c in range(8)]
        for hm in range(2):
            for c in range(8):
                k.mm(banks[0][:, 0:256], w_mkv[:, c, hm * 128:(hm + 1) * 128], memn[:, c, :], c == 0, c == 7, kw_mkv + memn_keys, ['bank0'])
            k.cp('act', mKT[:, hm, :], banks[0][:, 0:256], ['bank0'], ['mKT'])
            for mc in range(2):
                for c in range(8):
                    k.mm(banks[1][:, mc * 128:(mc + 1) * 128], memn[:, c, mc * 128:(mc + 1) * 128], w_mkv[:, c, 256 + hm * 128:256 + (hm + 1) * 128], c == 0, c == 7, kw_mkv + memn_keys, ['bank1'])
            for mc in range(2):
                k.cp('dve', mV[:, mc, hm, :], banks[1][:, mc * 128:(mc + 1) * 128], ['bank1'], ['mV'])
        xt = k.sb("xt", [128, 8, TN], F32, ph)
        pt = k.sb("pt", [128, 8, TN], F32, ph) if has_partials else None
        hT = k.sb("hT", [128, 8, TN], BF16, ph)
        cq_sb = k.sb("cq_sb", [128, 3, TN], F32, ph)
        ckv_sb = k.sb("ckv_sb", [128, 2, TN], F32, ph)
        cqn_t = k.sb("cqn_t", [128, 3, TN], BF16, ph)
        ckvn_t = k.sb("ckvn_t", [128, 2, TN], BF16, ph)
        qm = k.sb("qm", [128, 2, TN], BF16, ph)
        gate_t = k.sb("gate_t", [128, 8, TN], BF16, ph)
        rt1 = k.sb("rt1", [64, TN], F32, ph)
        rt2 = k.sb("rt2", [64, TN], F32, ph)
        pm = k.sb("pm", [128, 2, TN], BF16, ph)
        rD = k.sb("rD", [128, TN], F32, ph)
        yf = k.sb("yf", [128, TN], F32, ph)
        ym = k.sb("ym", [128, 2, TN], BF16, ph)
        mscale = 128.0 ** -0.5
        for t in range(NT):
            ts_ = slice(t * TN, (t + 1) * TN)
            S.dma(xt[:], xv[:, :, ts_], 'xt', writes=['xt'])
            if has_partials:
                for pi, pview in enumerate((p0v, p1v)):
                    S.dma(pt[:], pview[:, :, ts_], 'pt', writes=['pt'])
                    k.tt('pool' if pi == 0 else 'dve', xt[:], xt[:], pt[:], ALU.add, ['xt', 'pt'], ['xt'])
                S.dma(xnv[:, :, ts_], xt[:], 'xnst', reads=['xt'])
            rmsnorm_fm(k, [xt[:, c, :] for c in range(8)], ['xt'] * 8, gx, 1024, TN, ones, banks[3], sqr, tmpf,
                       [hT[:, c, :] for c in range(8)], [('hT', c) for c in range(8)])
            hkeys = [('hT', c) for c in range(8)]
            for j in range(16):
                b = j % 3
                bk = 'bank%d' % b
                for c in range(8):
                    k.mm(banks[b][:], w_in[:, c, j * 128:(j + 1) * 128], hT[:, c, :], c == 0, c == 7, kw_in + hkeys, [bk])
                if j < 3:
                    k.cp('act', cq_sb[:, j, :], banks[b][:], [bk], [('cq', j)])
                elif j < 5:
                    k.cp('dve', ckv_sb[:, j - 3, :], banks[b][:], [bk], [('ckv', j - 3)])
                elif j == 5:
                    k.tt('dve', rt1[:], banks[b][0:64, :], TAB[0:64, ts_], ALU.mult, [bk, tabkey(t * TN)], ['rt1'])
                    k.tt('dve', rt2[:], banks[b][64:128, :], TAB[64:128, ts_], ALU.mult, [bk, tabkey(t * TN)], ['rt2'])
                    k.tt('pool', KrT[:, ts_], rt1[:], rt2[:], ALU.add, ['rt1', 'rt2'], [('KrT', t)])
                elif j < 8:
                    k.cp('act', qm[:, j - 6, :], banks[b][:], [bk], [('qm', j - 6)])
                else:
                    k.act(gate_t[:, j - 8, :], banks[b][:], AF.Silu, [bk], [('gate', j - 8)])
                if j == 2:
                    rmsnorm_fm(k, [cq_sb[:, c, :] for c in range(3)], [('cq', c) for c in range(3)], gq, 384, TN, ones, banks[3], sqr, tmpf,
                               [cqn_t[:, c, :] for c in range(3)], [('cqn', c) for c in range(3)])
                    S.dma(cqnT[:, :, ts_], cqn_t[:], 'cqnst', reads=[('cqn', c) for c in range(3)], writes=[('cqnT', t)])
                if j == 4:
                    rmsnorm_fm(k, [ckv_sb[:, c, :] for c in range(2)], [('ckv', c) for c in range(2)], gkv, 256, TN, ones, banks[3], sqr, tmpf,
                               [ckvn_t[:, c, :] for c in range(2)], [('ckvn', c) for c in range(2)])
                    S.dma(ckvnT[:, :, ts_], ckvn_t[:], 'ckvnst', reads=[('ckvn', c) for c in range(2)], writes=[('ckvnT', t)])
                if j == 13:
                    S.dma(gateT[:, :, ts_], gate_t[:, 0:6, :], 'gatest', reads=[('gate', c) for c in range(6)], writes=[('gateT', t)])
            for hm in range(2):
                for mc in range(2):
                    k.mm(banks[4 + mc][:], mKT[:, hm, mc * 128:(mc + 1) * 128], qm[:, hm, :], True, True, ['mKT', ('qm', hm)], ['bank%d' % (4 + mc)])
                    k.act(pm[:, mc, :], banks[4 + mc][:], AF.Exp, ['bank%d' % (4 + mc)], [('pm', mc)], scale=mscale)
                for mc in range(2):
                    k.mm(banks[6][:], mV[:, mc, hm, :], pm[:, mc, :], mc == 0, mc == 1, ['mV', ('pm', mc)], ['bank6'])
                for mc in range(2):
                    k.mm(banks[7][:], ones[:], pm[:, mc, :], mc == 0, mc == 1, ['ones', ('pm', mc)], ['bank7'])
                S.op('dve', lambda e: e.reciprocal(rD[:], banks[7][:]), ['bank7'], ['rD'])
                k.tt('dve', yf[:], banks[6][:], rD[:], ALU.mult, ['bank6', 'rD'], ['yf'])
                k.tt('pool', ym[:, hm, :], yf[:], gate_t[:, 6 + hm, :], ALU.mult, ['yf', ('gate', 6 + hm)], [('ym', hm)])
            S.dma(yT[:, 6:8, ts_], ym[:], 'ymst', reads=[('ym', 0), ('ym', 1)], writes=[('yTm', t)])
        S.barrier()

    with ExitStack() as ph:
        w_uq = k.sb("w_uq_b", [128, 3, 1536], BF16, ph)
        w_ukv = k.sb("w_ukv_b", [128, 2, 1536], BF16, ph)
        load_w(k, w_uq, w_uq_d, 3, 1536, stg)
        load_w(k, w_ukv, w_ukv_d, 2, 1536, stg)
        kw_uq = wkeys(w_uq, 3, 1536); kw_ukv = wkeys(w_ukv, 2, 1536)
        KnT = k.sb("KnT", [128, SQ], BF16, ph)
        V = k.sb("V", [128, NB, 128], BF16, ph)
        ckvl = [k.sb("ckvl%d" % i, [128, 2, TN], BF16, ph) for i in range(2)]
        cql = [k.sb("cql%d" % i, [128, 3, TN], BF16, ph) for i in range(2)]
        QnT = [k.sb("QnT%d" % i, [128, TN], BF16, ph) for i in range(2)]
        QrT = [k.sb("QrT%d" % i, [64, TN], BF16, ph) for i in range(2)]
        rt1 = k.sb("rt1b", [64, TN], F32, ph)
        rt2 = k.sb("rt2b", [64, TN], F32, ph)
        PT = [k.sb("PT%d" % i, [128, TN], BF16, ph) for i in range(3)]
        rD = k.sb("rDb", [128, TN], F32, ph)
        yf = k.sb("yfb", [128, TN], F32, ph)
        gl = [k.sb("gl%d" % i, [128, TN], BF16, ph) for i in range(2)]
        yb = [k.sb("yb%d" % i, [128, TN], BF16, ph) for i in range(2)]
        scale = 192.0 ** -0.5
        OT, DD = banks[3], banks[4]
        pbi = 0
        stepi = 0
        for h in range(6):
            for t in range(NT):
                ts_ = slice(t * TN, (t + 1) * TN)
                s = t % 2
                S.dma(ckvl[s][:], ckvnT[:, :, ts_], 'ckvl%d' % s, reads=[('ckvnT', t)], writes=['ckvl%d' % s])
                b = 5 + pbi % 3; pbi += 1
                for c in range(2):
                    k.mm(banks[b][:], w_ukv[:, c, h * 256:h * 256 + 128], ckvl[s][:, c, :], c == 0, c == 1, kw_ukv + ['ckvl%d' % s], ['bank%d' % b])
                k.cp('act', KnT[:, ts_], banks[b][:], ['bank%d' % b], [('KnT', t)])
                b = 5 + pbi % 3; pbi += 1
                for blk in range(4):
                    for c in range(2):
                        k.mm(banks[b][:, blk * 128:(blk + 1) * 128], ckvl[s][:, c, blk * 128:(blk + 1) * 128], w_ukv[:, c, h * 256 + 128:h * 256 + 256], c == 0, c == 1, kw_ukv + ['ckvl%d' % s], ['bank%d' % b])
                k.cp('dve', V[:, t * 4:(t + 1) * 4, :], banks[b][:].rearrange("p (a b) -> p a b", a=4), ['bank%d' % b], [('V', t)])
            for g in range(NT):
                ts_ = slice(g * TN, (g + 1) * TN)
                s = g % 2
                S.dma(cql[s][:], cqnT[:, :, ts_], 'cql%d' % s, reads=[('cqnT', g)], writes=['cql%d' % s])
                S.dma(gl[s][:], gateT[:, h, ts_], 'gl%d' % s, reads=[('gateT', g)], writes=['gl%d' % s])
                b = 5 + pbi % 3; pbi += 1
                for c in range(3):
                    k.mm(banks[b][:], w_uq[:, c, h * 256:h * 256 + 128], cql[s][:, c, :], c == 0, c == 2, kw_uq + ['cql%d' % s], ['bank%d' % b])
                k.cp('act', QnT[s][:], banks[b][:], ['bank%d' % b], ['QnT%d' % s])
                b = 5 + pbi % 3; pbi += 1
                for c in range(3):
                    k.mm(banks[b][:], w_uq[:, c, h * 256 + 128:h * 256 + 256], cql[s][:, c, :], c == 0, c == 2, kw_uq + ['cql%d' % s], ['bank%d' % b])
                k.tt('dve', rt1[:], banks[b][0:64, :], TAB[0:64, ts_], ALU.mult, ['bank%d' % b, tabkey(g * TN)], ['rt1b'])
                k.tt('dve', rt2[:], banks[b][64:128, :], TAB[64:128, ts_], ALU.mult, ['bank%d' % b, tabkey(g * TN)], ['rt2b'])
                k.tt('pool', QrT[s][:], rt1[:], rt2[:], ALU.add, ['rt1b', 'rt2b'], ['QrT%d' % s])
                nst = 4 * g + 4

                def colsof(kb):
                    j = max(0, kb - 4 * g)
                    return j, slice(j * 128, TN)

                def emit_S(kb, si):
                    j, cs = colsof(kb)
                    bb = si % 3
                    kbs = slice(kb * 128, (kb + 1) * 128)
                    k.mm(banks[bb][:, cs], KnT[:, kbs], QnT[s][:, cs], True, False, [('KnT', kb // 4), 'QnT%d' % s], ['bank%d' % bb])
                    k.mm(banks[bb][:, cs], KrT[:, kbs], QrT[s][:, cs], False, True, [('KrT', kb // 4), 'QrT%d' % s], ['bank%d' % bb])

                def emit_PV(kb, si):
                    j, cs = colsof(kb)
                    bb = si % 3
                    k.act(PT[bb][:, cs], banks[bb][:, cs], AF.Exp, ['bank%d' % bb], ['PT%d' % bb], scale=scale)
                    if kb >= 4 * g:
                        dsl = slice(j * 128, (j + 1) * 128)
                        k.tt('pool', PT[bb][:, dsl], PT[bb][:, dsl], mask[:], ALU.mult, ['PT%d' % bb, 'mask'], ['PT%d' % bb])
                    k.mm(OT[:, cs], V[:, kb, :], PT[bb][:, cs], kb == 0, kb == nst - 1, [('V', kb // 4), 'PT%d' % bb], ['OT'])
                    k.mm(DD[:, cs], ones[:], PT[bb][:, cs], kb == 0, kb == nst - 1, ['ones', 'PT%d' % bb], ['DD'])

                emit_S(0, stepi)
                for kb in range(nst):
                    if kb + 1 < nst:
                        emit_S(kb + 1, stepi + kb + 1)
                    emit_PV(kb, stepi + kb)
                stepi += nst
                S.op('dve', lambda e: e.reciprocal(rD[:], DD[:]), ['DD'], ['rDb'])
                k.tt('dve', yf[:], OT[:], rD[:], ALU.mult, ['OT', 'rDb'], ['yfb'])
                k.tt('pool', yb[s][:], yf[:], gl[s][:], ALU.mult, ['yfb', 'gl%d' % s], ['yb%d' % s])
                S.dma(yT[:, h, ts_], yb[s][:], 'yb%d' % s, reads=['yb%d' % s], writes=[('yT', h, g)])
        S.barrier()

    with ExitStack() as ph:
        w_out = k.sb("w_out_b", [128, 8, 1024], BF16, ph)
        load_w(k, w_out, w_out_d, 8, 1024, stg)
        kw_out = wkeys(w_out, 8, 1024)
        yl = [k.sb("yl%d" % i, [128, 8, TN], BF16, ph) for i in range(2)]
        ot = [k.sb("ot%d" % i, [128, 8, TN], F32, ph) for i in range(2)]
        for t in range(NT):
            ts_ = slice(t * TN, (t + 1) * TN)
            s = t % 2
            S.dma(yl[s][:], yT[:, :, ts_], 'yl%d' % s, reads=[('yTm', t)] + [('yT', h, t) for h in range(6)], writes=['yl%d' % s])
            for oc in range(8):
                b = oc % 4
                for ic in range(8):
                    k.mm(banks[b][:], w_out[:, ic, oc * 128:(oc + 1) * 128], yl[s][:, ic, :], ic == 0, ic == 7, kw_out + ['yl%d' % s], ['bank%d' % b])
                k.cp('act' if oc % 2 == 0 else 'dve', ot[s][:, oc, :], banks[b][:], ['bank%d' % b], [('ot', s, oc)])
            S.dma(pv[:, :, ts_], ot[s][:], 'ot%d' % s, reads=[('ot', s, oc) for oc in range(8)])
        S.barrier()


TR = 256
EM = -0.6065306597126334


def rwkv_layer(k, nc, SQ):
    S = k.S
    NT = SQ // TR
    di = lambda n, sh, dt=F32: nc.dram_tensor(n, list(sh), dt, kind="ExternalInput").ap()
    do = lambda n, sh, dt=F32: nc.dram_tensor(n, list(sh), dt, kind="ExternalOutput").ap()
    xT = di("xT", [1024, SQ]); p0T = di("p0T", [1024, SQ]); p1T = di("p1T", [1024, SQ])
    xnT = do("xnT", [1024, SQ]); pT = do("pT", [1024, SQ])
    gains_d = di("gains", [128, 16]); mu_d = di("mu", [128, 19]); vec_d = di("vec", [128, 42])
    w_in_d = di("w_in", [1024, 3712]); w2a2_d = di("w2a2", [128, 768])
    w_out_d = di("w_out", [1024, 1024]); w_mkv_d = di("w_mkv", [1024, 512]); memT_d = di("memT", [1024, 256])
    xv = xT.rearrange("(c p) t -> p c t", p=128); p0v = p0T.rearrange("(c p) t -> p c t", p=128)
    p1v = p1T.rearrange("(c p) t -> p c t", p=128); xnv = xnT.rearrange("(c p) t -> p c t", p=128)
    pv = pT.rearrange("(c p) t -> p c t", p=128); memv = memT_d.rearrange("(c p) t -> p c t", p=128)

    banks = [k.ps("bank%d" % i, [128, 512], F32) for i in range(5)] + [None] + [k.ps("bank%d" % i, [128, 512], F32) for i in (6, 7)]
    tb = k.ps("tb", [128, 8, 128], BF16)
    sb = k.sb
    ones = sb("ones", [128, 128], BF16); bones = sb("bones", [128, 128], BF16)
    onesf = sb("onesf", [128, 128], F32); identf = sb("identf", [128, 128], F32); identb = sb("identb", [128, 128], BF16)
    m_sl = sb("m_sl", [128, 4, 128], F32); m_su = sb("m_su", [128, 4, 128], F32); m_iu = sb("m_iu", [128, 4, 128], F32)
    id4 = sb("id4", [128, 4, 128], F32)
    gains = sb("gainsb", [128, 16], F32); mu = sb("mub", [128, 19], F32); omu = sb("omub", [128, 19], F32); vec = sb("vecb", [128, 42], F32)
    k.epsb = sb("epsb", [128, 1], F32); tinyb = sb("tinyb", [128, 1], F32); gnepsb = sb("gnepsb", [128, 1], F32)
    sqr = [sb("sqr%d" % i, [128, 512], BF16) for i in range(2)]
    tmpf = sb("tmpf", [128, 512], F32)
    P = lambda f: S.op('pool', f[0], f[1], f[2])
    S.op('pool', lambda e: e.memset(ones[:], 1.0), writes=['ones'])
    S.op('pool', lambda e: e.memset(onesf[:], 1.0), writes=['onesf'])
    S.op('pool', lambda e: e.memset(bones[:], 0.0), writes=['bones'])
    S.op('pool', lambda e: e.memset(bones[0:64, 0:64], 1.0), ['bones'], ['bones'])
    S.op('pool', lambda e: e.memset(bones[64:128, 64:128], 1.0), ['bones'], ['bones'])
    S.op('pool', lambda e: e.memset(k.epsb[:], 1e-6), writes=['epsb'])
    S.op('pool', lambda e: e.memset(tinyb[:], 1e-30), writes=['tinyb'])
    S.op('pool', lambda e: e.memset(gnepsb[:], 64e-5), writes=['gnepsb'])
    for (tile_, pat, cm, op, nm) in ((m_sl, -1, 1, ALU.is_gt, 'm_sl'), (m_su, 1, -1, ALU.is_gt, 'm_su'), (m_iu, 1, -1, ALU.is_ge, 'm_iu')):
        S.op('pool', lambda e, tile_=tile_: e.memset(tile_[:], 1.0), writes=[nm])
        S.op('pool', lambda e, tile_=tile_, pat=pat, cm=cm, op=op: e.affine_select(out=tile_[:], in_=tile_[:], pattern=[[0, 4], [pat, 128]], compare_op=op, fill=0.0, base=0, channel_multiplier=cm), [nm], [nm])
    S.op('pool', lambda e: e.memset(id4[:], 0.0), writes=['id4'])
    S.op('pool', lambda e: e.affine_select(out=id4[:], in_=id4[:], pattern=[[0, 4], [-1, 128]], compare_op=ALU.not_equal, fill=1.0, base=0, channel_multiplier=1), ['id4'], ['id4'])
    k.cp('dve', identb[:], id4[:, 0, :], ['id4'], ['identb'])
    S.dma(gains[:], gains_d, 'c_gains', writes=['gains'])
    S.dma(mu[:], mu_d, 'c_mu', writes=['mu'])
    S.dma(vec[:], vec_d, 'c_vec', writes=['vec'])
    k.ts('dve', omu[:], mu[:], -1.0, 1.0, ALU.mult, ALU.add, ['mu'], ['omu'])
    gx, gm = gains[:, 0:8], gains[:, 8:16]
    V_ = lambda i, p: vec[:, i * 6 + p:i * 6 + p + 1]

    w_in = sb("w_in_b", [128, 8, 3712], BF16)
    w_out = sb("w_out_b", [128, 8, 1024], BF16)
    w2a2 = sb("w2a2_b", [128, 768], BF16)
    mKT = sb("mKT", [128, 2, 256], BF16); mV = sb("mV", [128, 2, 2, 128], BF16)
    with ExitStack() as ph:
        stg = [k.sb("stg%d" % i, [128, 2048], F32, ph) for i in range(2)]
        w_mkv = k.sb("w_mkv_b", [128, 8, 512], BF16, ph)
        load_w(k, w_mkv, w_mkv_d, 8, 512, stg)
        load_w(k, w_in, w_in_d, 8, 3712, stg)
        load_w(k, w_out, w_out_d, 8, 1024, stg)
        S.dma(stg[0][:, 0:768], w2a2_d, 'stg0', writes=['stg0'])
        k.cp('dve', w2a2[:], stg[0][:, 0:768], ['stg0'], ['w2a2'])
        kw_mkv = wkeys(w_mkv, 8, 512)
        memx = k.sb("memx", [128, 8, 256], F32, ph); memn = k.sb("memn", [128, 8, 256], BF16, ph)
        S.dma(memx[:], memv, 'memx', writes=['memx'])
        rmsnorm_fm(k, [memx[:, c, :] for c in range(8)], ['memx'] * 8, gm, 1024, 256, ones, banks[3], sqr, tmpf,
                   [memn[:, c, :] for c in range(8)], [('memn', c) for c in range(8)])
        memn_keys = [('memn', c) for c in range(8)]
        for hm in range(2):
            for c in range(8):
                k.mm(banks[0][:, 0:256], w_mkv[:, c, hm * 128:(hm + 1) * 128], memn[:, c, :], c == 0, c == 7, kw_mkv + memn_keys, ['bank0'])
            k.cp('act', mKT[:, hm, :], banks[0][:, 0:256], ['bank0'], ['mKT'])
            for mc in range(2):
                for c in range(8):
                    k.mm(banks[1][:, mc * 128:(mc + 1) * 128], memn[:, c, mc * 128:(mc + 1) * 128], w_mkv[:, c, 256 + hm * 128:256 + (hm + 1) * 128], c == 0, c == 7, kw_mkv + memn_keys, ['bank1'])
            for mc in range(2):
                k.cp('dve', mV[:, mc, hm, :], banks[1][:, mc * 128:(mc + 1) * 128], ['bank1'], ['mV'])
        S.barrier()
    kw_in = wkeys(w_in, 8, 3712); kw_out = wkeys(w_out, 8, 1024)

    N = TR
    xt = sb("xt", [128, 8, N], F32); pt = sb("pt", [128, 8, N], F32); hT = sb("hT", [128, 8, N], BF16)
    Ub = [sb("Ub%d" % i, [128, N + 1], F32) for i in range(2)]
    lastcol = sb("lastcol", [128, 19], F32)
    S.op('pool', lambda e: e.memset(lastcol[:], 0.0), writes=['lastcol'])
    shtmp = sb("shtmp", [128, N], F32)
    r_t = sb("r_t", [128, 6, N], F32); k_t = sb("k_t", [128, 6, N], F32); v_t = sb("v_t", [128, 6, N], F32)
    wdad = sb("wdad", [128, N], F32); TA = sb("TA", [128, N], BF16)
    qm = sb("qm", [128, 2, N], BF16); gate_t = sb("gate_t", [128, 8, N], BF16)
    y_t = sb("y_t", [128, 8, N], BF16); ot = sb("ot", [128, 8, N], F32)
    pm_ = sb("pm", [128, 2, N], BF16); rD = sb("rD", [128, N], F32); yf = sb("yf", [128, N], F32)
    S32 = [sb("S32_%d" % p, [128, 128], F32) for p in range(6)]
    S16 = [sb("S16_%d" % p, [128, 128], BF16) for p in range(6)]
    for p in range(6):
        S.op('pool', lambda e, p=p: e.memset(S32[p][:], 0.0), writes=['S32_%d' % p])
        S.op('pool', lambda e, p=p: e.memset(S16[p][:], 0.0), writes=['S16_%d' % p])
    fT = {n: sb("f_" + n, [128, N], F32) for n in ('lw', 'aP', 'kkr', 'kk', 't1', 'k2', 'rk', 'cl', 'clp', 'epos', 'eneg', 'eprev', 'b', 'bon', 'yraw', 'd', 'yn')}
    hT16 = {n: sb("h_" + n, [128, N], BF16) for n in ('rkk', 'vb', 'BT', 'KT', 'yb', 'dsq')}
    AH = [sb("AH%d" % i, [128, N], BF16) for i in range(2)]; BH = [sb("BH%d" % i, [128, N], BF16) for i in range(2)]
    KH = [sb("KH%d" % i, [128, N], BF16) for i in range(2)]; RH = [sb("RH%d" % i, [128, N], BF16) for i in range(2)]
    TT = [sb("TT%d" % i, [128, 6, 128], BF16) for i in range(2)]
    Vpad = [sb("Vpad%d" % i, [128, 2, 2, 128], BF16) for i in range(2)]
    Upad = [sb("Upad%d" % i, [128, 2, 128], BF16) for i in range(2)]
    for i in range(2):
        S.op('pool', lambda e, i=i: e.memset(Vpad[i][:], 0.0), writes=['Vpad%d' % i])
        S.op('pool', lambda e, i=i: e.memset(Upad[i][:], 0.0), writes=['Upad%d' % i])
    Zs = [sb("Zs%d" % i, [128, 128], BF16) for i in range(2)]; Us = [sb("Us%d" % i, [128, 128], BF16) for i in range(2)]
    Pm = [sb("Pm%d" % i, [128, 4, 128], BF16) for i in range(2)]; Qm = [sb("Qm%d" % i, [128, 4, 128], BF16) for i in range(2)]
    Ym = [sb("Ym%d" % i, [128, 4, 128], BF16) for i in range(3)]
    LAKT = [sb("LAKT%d" % i, [128, 4, 128], BF16) for i in range(2)]
    MRBT = [sb("MRBT%d" % i, [128, 4, 128], BF16) for i in range(2)]
    MRKT = [sb("MRKT%d" % i, [128, 4, 128], BF16) for i in range(2)]
    mscale = 128.0 ** -0.5
    st_ = {'gb': 0, 'mb': 0, 'pc': 0, 'zs': 0}

    def gbank():
        b = st_['gb'] % 2
        st_['gb'] += 1
        return banks[b], 'bank%d' % b

    def mbank():
        b = 2 + st_['mb'] % 3
        st_['mb'] += 1
        return banks[b], 'bank%d' % b

    for t in range(NT):
        ts_ = slice(t * N, (t + 1) * N)
        S.dma(xt[:], xv[:, :, ts_], 'xt', writes=['xt'])
        for pi, pview in enumerate((p0v, p1v)):
            S.dma(pt[:], pview[:, :, ts_], 'pt', writes=['pt'])
            k.tt('pool' if pi == 0 else 'dve', xt[:], xt[:], pt[:], ALU.add, ['xt', 'pt'], ['xt'])
        S.dma(xnv[:, :, ts_], xt[:], 'xnst', reads=['xt'])
        rmsnorm_fm(k, [xt[:, c, :] for c in range(8)], ['xt'] * 8, gx, 1024, N, ones, banks[4], sqr, tmpf,
                   [hT[:, c, :] for c in range(8)], [('hT', c) for c in range(8)])
        hkeys = [('hT', c) for c in range(8)]
        for j in range(29):
            bk, bkey = gbank()
            for c in range(8):
                k.mm(bk[:, 0:N], w_in[:, c, j * 128:(j + 1) * 128], hT[:, c, :], c == 0, c == 7, kw_in + hkeys, [bkey])
            if j < 19:
                u = Ub[j % 2]; ukey = 'Ub%d' % (j % 2)
                k.cp('pool', u[:, 0:1], lastcol[:, j:j + 1], ['lastcol'], [ukey])
                k.cp('act', u[:, 1:N + 1], bk[:, 0:N], [bkey, ukey], [ukey])
                k.cp('pool', lastcol[:, j:j + 1], u[:, N:N + 1], [ukey], ['lastcol'])
                k.ts('dve', shtmp[:], u[:, 1:N + 1], omu[:, j:j + 1], None, ALU.mult, None, [ukey, 'omu'], ['shtmp'])
                if j < 18:
                    dst = (r_t, k_t, v_t)[j // 6][:, j % 6, :]
                    dkey = ('rkv', j // 6, j % 6)
                else:
                    dst = wdad[:]; dkey = 'wdad'
                k.stt('dve', dst, u[:, 0:N], mu[:, j:j + 1], shtmp[:], ALU.mult, ALU.add, [ukey, 'mu', 'shtmp'], [dkey])
            elif j < 21:
                k.cp('act', qm[:, j - 19, :], bk[:, 0:N], [bkey], [('qm', j - 19)])
            else:
                k.act(gate_t[:, j - 21, :], bk[:, 0:N], AF.Silu, [bkey], [('gate', j - 21)])
        k.act(TA[0:64, :], wdad[0:64, :], AF.Tanh, ['wdad'], ['TA'])
        k.cp('act', TA[64:128, :], wdad[64:128, :], ['wdad', 'TA'], ['TA'])

        for p in range(6):
            s2 = st_['pc'] % 2
            st_['pc'] += 1
            f = fT; hq = hT16
            rP, kP, vP = r_t[:, p, :], k_t[:, p, :], v_t[:, p, :]
            kr, kk_, kv = ('rkv', 0, p), ('rkv', 1, p), ('rkv', 2, p)
            ck = slice(p * 128, (p + 1) * 128)
            bk, bkey = gbank()
            k.mm(bk[:, 0:N], w2a2[0:64, ck], TA[0:64, :], True, True, ['w2a2', 'TA'], [bkey])
            k.act(f['lw'][:], bk[:, 0:N], AF.Sigmoid, [bkey, 'vec'], ['lw'], bias=V_(0, p))
            k.ts('dve', f['lw'][:], f['lw'][:], EM, None, ALU.mult, None, ['lw'], ['lw'])
            bk, bkey = gbank()
            k.mm(bk[:, 0:N], w2a2[64:128, ck], TA[64:128, :], True, True, ['w2a2', 'TA'], [bkey])
            k.act(f['aP'][:], bk[:, 0:N], AF.Sigmoid, [bkey, 'vec'], ['aP'], bias=V_(1, p))
            k.ts('dve', f['kkr'][:], kP, V_(2, p), None, ALU.mult, None, [kk_, 'vec'], ['kkr'])
            k.act(hq['dsq'][:], f['kkr'][:], AF.Square, ['kkr'], ['dsq'])
            bk, bkey = gbank()
            k.mm(bk[:, 0:N], bones[:], hq['dsq'][:], True, True, ['bones', 'dsq'], [bkey])
            k.act(f['t1'][:], bk[:, 0:N], AF.Ln, [bkey, 'tinyb'], ['t1'], bias=tinyb[:])
            k.act(f['t1'][:], f['t1'][:], AF.Exp, ['t1'], ['t1'], scale=-0.5)
            k.tt('pool', f['kk'][:], f['kkr'][:], f['t1'][:], ALU.mult, ['kkr', 't1'], ['kk'])
            k.ts('dve', f['t1'][:], f['aP'][:], -1.0, V_(3, p), ALU.add, ALU.mult, ['aP', 'vec', 't1', 'kk'], ['t1'])
            k.stt('dve', f['k2'][:], f['t1'][:], 1.0, kP, ALU.add, ALU.mult, ['t1', kk_], ['k2'])
            k.ts('dve', f['rk'][:], rP, V_(4, p), None, ALU.mult, None, [kr, 'vec'], ['rk'])
            k.tt('pool', hq['rkk'][:], f['rk'][:], f['k2'][:], ALU.mult, ['rk', 'k2'], ['rkk'])
            bk, bkey = gbank()
            k.mm(bk[:, 0:N], bones[:], hq['rkk'][:], True, True, ['bones', 'rkk'], [bkey])
            k.tt('dve', f['bon'][:], bk[:, 0:N], vP, ALU.mult, [bkey, kv], ['bon'])
            for c in range(2):
                cs = slice(c * 128, (c + 1) * 128)
                S.op('dve', lambda e, cs=cs: e.tensor_tensor_scan(f['cl'][:, cs], onesf[:, 0:128], f['lw'][:, cs], 0.0, ALU.mult, ALU.add), ['onesf', 'lw'], ['cl'])
            k.tt('pool', f['clp'][:], f['cl'][:], f['lw'][:], ALU.subtract, ['cl', 'lw'], ['clp'])
            k.act(f['epos'][:], f['cl'][:], AF.Exp, ['cl'], ['epos'])
            k.act(f['eneg'][:], f['cl'][:], AF.Exp, ['cl'], ['eneg'], scale=-1.0)
            k.act(f['eprev'][:], f['clp'][:], AF.Exp, ['clp'], ['eprev'])
            k.stt('dve', AH[s2][:], f['kk'][:], -1.0, f['eprev'][:], ALU.mult, ALU.mult, ['kk', 'eprev'], ['AH%d' % s2])
            k.tt('pool', f['b'][:], f['kk'][:], f['aP'][:], ALU.mult, ['kk', 'aP'], ['b'])
            k.tt('pool', BH[s2][:], f['b'][:], f['eneg'][:], ALU.mult, ['b', 'eneg'], ['BH%d' % s2])
            k.tt('dve', KH[s2][:], f['k2'][:], f['eneg'][:], ALU.mult, ['k2', 'eneg'], ['KH%d' % s2])
            k.tt('pool', RH[s2][:], rP, f['epos'][:], ALU.mult, [kr, 'epos'], ['RH%d' % s2])
            for c in range(2):
                cs = slice(c * 128, (c + 1) * 128)
                wc = f['epos'][:, c * 128 + 127:c * 128 + 128]
                k.ts('dve', hq['BT'][:, cs], BH[s2][:, cs], wc, None, ALU.mult, None, ['BH%d' % s2, 'epos'], ['BT'])
                k.ts('dve', hq['KT'][:, cs], KH[s2][:, cs], wc, None, ALU.mult, None, ['KH%d' % s2, 'epos'], ['KT'])
            k.cp('act', hq['vb'][:], vP, [kv], ['vb'])
            for c in range(2):
                cs = slice(c * 128, (c + 1) * 128)
                S.op('pe', lambda e, c=c, cs=cs: e.transpose(tb[:, c, :], hq['vb'][:, cs], identb[:]), ['vb', 'identb'], ['tb'])
                S.op('pe', lambda e, c=c, cs=cs: e.transpose(tb[:, 2 + c, :], hq['BT'][:, cs], identb[:]), ['BT', 'identb'], ['tb'])
                S.op('pe', lambda e, c=c, cs=cs: e.transpose(tb[:, 4 + c, :], hq['KT'][:, cs], identb[:]), ['KT', 'identb'], ['tb'])
            k.cp('act', TT[s2][:], tb[:, 0:6, :], ['tb'], ['TT%d' % s2])
            k.cp('pool', Vpad[s2][:, :, 0, 0:64], TT[s2][:, 0:2, 0:64], ['TT%d' % s2], ['Vpad%d' % s2])
            k.cp('pool', Vpad[s2][:, :, 1, 64:128], TT[s2][:, 0:2, 64:128], ['TT%d' % s2], ['Vpad%d' % s2])
            def mat(lh, lkey, rh, rkey, mask, mkey, dst, dkey):
                for hd in range(2):
                    bk, bkey = banks[2 + hd], 'bank%d' % (2 + hd)
                    pr = slice(hd * 64, (hd + 1) * 64)
                    for c in range(2):
                        cs = slice(c * 128, (c + 1) * 128)
                        k.mm(bk[:, c * 128:(c + 1) * 128], lh[pr, cs], rh[pr, cs], True, True, [lkey, rkey], [bkey])
                    k.tt('dve', dst[:, hd * 2:hd * 2 + 2, :], bk[:, 0:256].rearrange("p (a b) -> p a b", a=2), mask[:, 0:2, :], ALU.mult, [bkey, mkey], [dkey])

            a_, b_, k_, r_ = AH[s2], BH[s2], KH[s2], RH[s2]
            ak, bkk, kkk, rk_ = 'AH%d' % s2, 'BH%d' % s2, 'KH%d' % s2, 'RH%d' % s2
            mat(a_, ak, b_, bkk, m_sl, 'm_sl', Pm[0], 'Pm0')
            mat(b_, bkk, a_, ak, m_su, 'm_su', Qm[0], 'Qm0')
            mat(k_, kkk, a_, ak, m_su, 'm_su', LAKT[s2], 'LAKT%d' % s2)
            mat(b_, bkk, r_, rk_, m_iu, 'm_iu', MRBT[s2], 'MRBT%d' % s2)
            mat(k_, kkk, r_, rk_, m_iu, 'm_iu', MRKT[s2], 'MRKT%d' % s2)
            k.tt('pool', Ym[0][:], Qm[0][:], id4[:], ALU.add, ['Qm0', 'id4'], ['Ym0'])
            pi_, qi_, yi_ = 0, 0, 0
            for lvl in range(1, 7):
                pn = 1 - pi_
                bk, bkey = mbank()
                for sbi in range(4):
                    k.mm(bk[:, sbi * 128:(sbi + 1) * 128], Qm[qi_][:, sbi, :], Pm[pi_][:, sbi, :], True, True, ['Qm%d' % qi_, 'Pm%d' % pi_], [bkey])
                k.cp('act', Pm[pn][:], bk[:].rearrange("p (a b) -> p a b", a=4), [bkey], ['Pm%d' % pn])
                if lvl <= 5:
                    qn = 1 - qi_
                    bk, bkey = mbank()
                    for sbi in range(4):
                        k.mm(bk[:, sbi * 128:(sbi + 1) * 128], Pm[pi_][:, sbi, :], Qm[qi_][:, sbi, :], True, True, ['Qm%d' % qi_, 'Pm%d' % pi_], [bkey])
                    k.cp('act', Qm[qn][:], bk[:].rearrange("p (a b) -> p a b", a=4), [bkey], ['Qm%d' % qn])
                    qi_ = qn
                pi_ = pn
                yn_ = (yi_ + 1) % 3
                bk, bkey = mbank()
                for sbi in range(4):
                    k.mm(bk[:, sbi * 128:(sbi + 1) * 128], Pm[pi_][:, sbi, :], Ym[yi_][:, sbi, :], True, True, ['Pm%d' % pi_, 'Ym%d' % yi_], [bkey])
                k.tt('dve', Ym[yn_][:], bk[:].rearrange("p (a b) -> p a b", a=4), Ym[yi_][:], ALU.add, [bkey, 'Ym%d' % yi_], ['Ym%d' % yn_])
                yi_ = yn_
            XT = Ym[yi_]; xk = 'Ym%d' % yi_
            skey16, skey32 = 'S16_%d' % p, 'S32_%d' % p
            for c in range(2):
                cs = slice(c * 128, (c + 1) * 128)
                zs = st_['zs'] % 2
                st_['zs'] += 1
                zb = banks[6][:, zs * 128:(zs + 1) * 128]; zkey = ('z', zs)
                ub = banks[7][:, zs * 128:(zs + 1) * 128]; ukey = ('u', zs)
                yb_ = banks[6][:, 256 + zs * 128:256 + (zs + 1) * 128]; ykey = ('y', zs)
                sb_ = banks[7][:, 256 + zs * 128:256 + (zs + 1) * 128]; skey = ('s', zs)
                k.mm(zb, a_[:, cs], S16[p][:], True, False, [ak, skey16], [zkey])
                k.mm(zb[:, 0:64], LAKT[s2][:, c, :], TT[s2][:, c, 0:64], False, False, ['LAKT%d' % s2, 'TT%d' % s2], [zkey])
                k.mm(zb[:, 64:128], LAKT[s2][:, 2 + c, :], TT[s2][:, c, 64:128], False, True, ['LAKT%d' % s2, 'TT%d' % s2], [zkey])
                k.cp('act', Zs[zs][:], zb, [zkey], ['Zs%d' % zs])
                k.mm(ub[:, 0:64], XT[:, c, :], Zs[zs][:, 0:64], True, True, [xk, 'Zs%d' % zs], [ukey])
                k.mm(ub[:, 64:128], XT[:, 2 + c, :], Zs[zs][:, 64:128], True, True, [xk, 'Zs%d' % zs], [ukey])
                k.cp('act', Us[zs][:], ub, [ukey], ['Us%d' % zs])
                k.cp('pool', Upad[zs][:, 0, 0:64], Us[zs][:, 0:64], ['Us%d' % zs], ['Upad%d' % zs])
                k.cp('pool', Upad[zs][:, 1, 64:128], Us[zs][:, 64:128], ['Us%d' % zs], ['Upad%d' % zs])
                k.mm(yb_, S16[p][:], r_[:, cs], True, False, [skey16, rk_], [ykey])
                k.mm(yb_, Upad[zs][:, 0, :], MRBT[s2][:, c, :], False, False, ['Upad%d' % zs, 'MRBT%d' % s2], [ykey])
                k.mm(yb_, Upad[zs][:, 1, :], MRBT[s2][:, 2 + c, :], False, False, ['Upad%d' % zs, 'MRBT%d' % s2], [ykey])
                k.mm(yb_, Vpad[s2][:, c, 0, :], MRKT[s2][:, c, :], False, False, ['Vpad%d' % s2, 'MRKT%d' % s2], [ykey])
                k.mm(yb_, Vpad[s2][:, c, 1, :], MRKT[s2][:, 2 + c, :], False, True, ['Vpad%d' % s2, 'MRKT%d' % s2], [ykey])
                k.cp('act', f['yraw'][:, cs], yb_, [ykey], ['yraw'])
                k.mm(sb_, TT[s2][:, 2 + c, :], Us[zs][:], True, False, ['TT%d' % s2, 'Us%d' % zs], [skey])
                k.mm(sb_, TT[s2][:, 4 + c, :], TT[s2][:, c, :], False, True, ['TT%d' % s2], [skey])
                for hd in range(2):
                    pr = slice(hd * 64, (hd + 1) * 64)
                    k.stt('dve', S32[p][pr, pr], S32[p][pr, pr], f['epos'][pr, c * 128 + 127:c * 128 + 128], sb_[pr, pr], ALU.mult, ALU.add,
                          [skey32, 'epos', skey], [skey32])
                k.cp('pool', S16[p][:], S32[p][:], [skey32], [skey16])
            k.cp('pool', hq['yb'][:], f['yraw'][:], ['yraw'], ['yb'])
            bk, bkey = gbank()
            k.mm(bk[:, 0:N], bones[:], hq['yb'][:], True, True, ['bones', 'yb'], [bkey])
            k.stt('dve', f['d'][:], bk[:, 0:N], -1.0 / 64, f['yraw'][:], ALU.mult, ALU.add, [bkey, 'yraw'], ['d'])
            k.act(hq['dsq'][:], f['d'][:], AF.Square, ['d'], ['dsq'])
            bk, bkey = gbank()
            k.mm(bk[:, 0:N], bones[:], hq['dsq'][:], True, True, ['bones', 'dsq'], [bkey])
            k.act(f['yn'][:], bk[:, 0:N], AF.Ln, [bkey, 'gnepsb'], ['yn'], scale=1.0 / 64, bias=gnepsb[:])
            k.act(f['yn'][:], f['yn'][:], AF.Exp, ['yn'], ['yn'], scale=-0.5)
            k.tt('pool', f['yn'][:], f['yn'][:], f['d'][:], ALU.mult, ['yn', 'd'], ['yn'])
            k.ts('dve', f['yn'][:], f['yn'][:], V_(5, p), V_(6, p), ALU.mult, ALU.add, ['yn', 'vec'], ['yn'])
            k.tt('pool', f['yn'][:], f['yn'][:], f['bon'][:], ALU.add, ['yn', 'bon'], ['yn'])
            k.tt('pool', y_t[:, p, :], f['yn'][:], gate_t[:, p, :], ALU.mult, ['yn', ('gate', p)], [('y', p)])
        for hm in range(2):
            for mc in range(2):
                bk, bkey = banks[2 + mc], 'bank%d' % (2 + mc)
                k.mm(bk[:, 0:N], mKT[:, hm, mc * 128:(mc + 1) * 128], qm[:, hm, :], True, True, ['mKT', ('qm', hm)], [bkey])
                k.act(pm_[:, mc, :], bk[:, 0:N], AF.Exp, [bkey], [('pm', mc)], scale=mscale)
            for mc in range(2):
                k.mm(banks[4][:, 0:N], mV[:, mc, hm, :], pm_[:, mc, :], mc == 0, mc == 1, ['mV', ('pm', mc)], ['bank4'])
            bk, bkey = gbank()
            for mc in range(2):
                k.mm(bk[:, 0:N], ones[:], pm_[:, mc, :], mc == 0, mc == 1, ['ones', ('pm', mc)], [bkey])
            S.op('dve', lambda e, bk=bk: e.reciprocal(rD[:], bk[:, 0:N]), [bkey], ['rD'])
            k.tt('dve', yf[:], banks[4][:, 0:N], rD[:], ALU.mult, ['bank4', 'rD'], ['yf'])
            k.tt('pool', y_t[:, 6 + hm, :], yf[:], gate_t[:, 6 + hm, :], ALU.mult, ['yf', ('gate', 6 + hm)], [('y', 6 + hm)])
        ykeys = [('y', i) for i in range(8)]
        for oc in range(8):
            bk, bkey = gbank()
            for ic in range(8):
                k.mm(bk[:, 0:N], w_out[:, ic, oc * 128:(oc + 1) * 128], y_t[:, ic, :], ic == 0, ic == 7, kw_out + ykeys, [bkey])
            k.cp('act' if oc % 2 == 0 else 'dve', ot[:, oc, :], bk[:, 0:N], [bkey], [('ot', oc)])
        S.dma(pv[:, :, ts_], ot[:], 'otst', reads=[('ot', oc) for oc in range(8)])
    S.barrier()


PERM = np.concatenate([np.arange(0, 64, 2), np.arange(1, 64, 2)])
PERM_SW = np.concatenate([np.arange(1, 64, 2), np.arange(0, 64, 2)])


def pm(v, C):
    return np.ascontiguousarray(np.asarray(v, np.float32).reshape(C, 128).T)


def ropec_tab():
    invf = (np.float32(10000.0) ** (-(np.arange(0, 64, 2, dtype=np.float32)) / np.float32(64))).astype(np.float32)
    t = np.zeros((128, 4), np.float32)
    p = np.arange(128)
    t[:, 0] = invf[p % 32]
    t[:64, 1] = np.pi / 2
    t[:64, 2] = 1
    t[64:96, 2] = -1
    t[96:, 2] = 1
    return t


def prep_mla(inp, i, b, hh, SQ):
    j = i // 2
    w = np.asarray(inp['w_in_mla'][j], np.float32)
    kr = w[:, 640:704]
    gate = w[:, 1216:3264]
    w_in = np.concatenate([w[:, 0:640], kr[:, PERM], kr[:, PERM_SW], w[:, 704 + hh * 256:704 + (hh + 1) * 256],
                           gate[:, hh * 768:(hh + 1) * 768], gate[:, 1536 + hh * 256:1536 + (hh + 1) * 256]], axis=1)
    wq = np.asarray(inp['mla_w_uq'][j], np.float32).reshape(384, 12, 192)[:, hh * 6:(hh + 1) * 6]
    w_uq = np.concatenate([wq[:, :, 0:128], wq[:, :, 128:192][:, :, PERM], wq[:, :, 128:192][:, :, PERM_SW]], axis=2).reshape(384, 1536)
    w_ukv = np.asarray(inp['mla_w_ukv'][j], np.float32)[:, hh * 1536:(hh + 1) * 1536]
    wo = np.asarray(inp['w_out'][i], np.float32)
    w_out = np.concatenate([wo[hh * 768:(hh + 1) * 768], wo[1536 + hh * 256:1536 + (hh + 1) * 256]], axis=0)
    wm = np.asarray(inp['w_mem_kv'][i], np.float32)
    w_mkv = np.concatenate([wm[:, hh * 256:(hh + 1) * 256], wm[:, 512 + hh * 256:512 + (hh + 1) * 256]], axis=1)
    gains = np.concatenate([pm(inp['norm_g'][i], 8), pm(inp['mla_q_norm_g'][j], 3), pm(inp['mla_kv_norm_g'][j], 2), pm(inp['mem_norm_g'][i], 8)], axis=1)
    c = np.ascontiguousarray
    return {
        'gains': c(gains), 'w_in': c(w_in), 'w_uq': c(w_uq), 'w_ukv': c(w_ukv), 'w_out': c(w_out), 'w_mkv': c(w_mkv),
        'memT': c(np.asarray(inp['mem'][b], np.float32).T), 'pos': c(np.asarray(inp['positions'][b][:SQ], np.int32)),
        'ropec': ropec_tab(),
    }


def prep_rwkv(inp, i, b, hh):
    j = i // 2
    w = np.asarray(inp['w_in_rwkv'][j], np.float32)
    own = slice(hh * 768, (hh + 1) * 768)
    gate = w[:, 5248:7296]
    w_in = np.concatenate([w[:, 0:1536][:, own], w[:, 1536:3072][:, own], w[:, 3072:4608][:, own], w[:, 4608:4736],
                           w[:, 4736 + hh * 256:4736 + (hh + 1) * 256], gate[:, hh * 768:(hh + 1) * 768], gate[:, 1536 + hh * 256:1536 + (hh + 1) * 256]], axis=1)
    mu = np.asarray(inp['rwkv_mu'][j], np.float32)
    mu_d = np.concatenate([mu[0:1536][own], mu[1536:3072][own], mu[3072:4608][own], mu[4608:4736]])
    vec = np.concatenate([pm(np.asarray(inp[n][j], np.float32)[own], 6) for n in ('rwkv_w0', 'rwkv_a0', 'rwkv_k_k', 'rwkv_k_a', 'rwkv_r_k', 'rwkv_gn_w', 'rwkv_gn_b')], axis=1)
    w2a2 = np.concatenate([np.asarray(inp['rwkv_w2'][j], np.float32)[:, own], np.asarray(inp['rwkv_a2'][j], np.float32)[:, own]], axis=0)
    wo = np.asarray(inp['w_out'][i], np.float32)
    w_out = np.concatenate([wo[hh * 768:(hh + 1) * 768], wo[1536 + hh * 256:1536 + (hh + 1) * 256]], axis=0)
    wm = np.asarray(inp['w_mem_kv'][i], np.float32)
    w_mkv = np.concatenate([wm[:, hh * 256:(hh + 1) * 256], wm[:, 512 + hh * 256:512 + (hh + 1) * 256]], axis=1)
    gains = np.concatenate([pm(inp['norm_g'][i], 8), pm(inp['mem_norm_g'][i], 8)], axis=1)
    c = np.ascontiguousarray
    return {'gains': c(gains), 'mu': pm(mu_d, 19), 'vec': c(vec), 'w_in': c(w_in), 'w2a2': c(w2a2), 'w_out': c(w_out), 'w_mkv': c(w_mkv),
            'memT': c(np.asarray(inp['mem'][b], np.float32).T)}


def final_layer(k, nc, SQ):
    S = k.S
    NT = SQ // TN
    di = lambda n, sh, dt=F32: nc.dram_tensor(n, list(sh), dt, kind="ExternalInput").ap()
    xT = di("xT", [1024, SQ]); p0T = di("p0T", [1024, SQ]); p1T = di("p1T", [1024, SQ])
    gains_d = di("gains", [128, 8])
    oT = nc.dram_tensor("oT", [1024, SQ], F32, kind="ExternalOutput").ap()
    xv = xT.rearrange("(c p) t -> p c t", p=128); p0v = p0T.rearrange("(c p) t -> p c t", p=128)
    p1v = p1T.rearrange("(c p) t -> p c t", p=128); ov = oT.rearrange("(c p) t -> p c t", p=128)
    bank = k.ps("bank0", [128, 512], F32)
    ones = k.sb("ones", [128, 128], BF16)
    gains = k.sb("gainsb", [128, 8], F32)
    k.epsb = k.sb("epsb", [128, 1], F32)
    sqr = [k.sb("sqr%d" % i, [128, 512], BF16) for i in range(2)]
    tmpf = k.sb("tmpf", [128, 512], F32)
    S.op('pool', lambda e: e.memset(ones[:], 1.0), writes=['ones'])
    S.op('pool', lambda e: e.memset(k.epsb[:], 1e-6), writes=['epsb'])
    S.dma(gains[:], gains_d, 'c_gains', writes=['gains'])
    xt = [k.sb("xt%d" % i, [128, 8, TN], F32) for i in range(2)]
    pt = [k.sb("pt%d" % i, [128, 8, TN], F32) for i in range(2)]
    ot = [k.sb("ot%d" % i, [128, 8, TN], F32) for i in range(2)]
    for t in range(NT):
        ts_ = slice(t * TN, (t + 1) * TN)
        s = t % 2
        S.dma(xt[s][:], xv[:, :, ts_], 'xt%d' % s, writes=['xt%d' % s])
        for pi, pview in enumerate((p0v, p1v)):
            S.dma(pt[pi][:], pview[:, :, ts_], 'pt%d' % pi, writes=['pt%d' % pi])
            k.tt('pool' if pi == 0 else 'dve', xt[s][:], xt[s][:], pt[pi][:], ALU.add, ['xt%d' % s, 'pt%d' % pi], ['xt%d' % s])
        rmsnorm_fm(k, [xt[s][:, c, :] for c in range(8)], ['xt%d' % s] * 8, gains, 1024, TN, ones, bank, sqr, tmpf,
                   [ot[s][:, c, :] for c in range(8)], [('ot', s, c) for c in range(8)])
        S.dma(ov[:, :, ts_], ot[s][:], 'ot%d' % s, reads=[('ot', s, c) for c in range(8)])
    S.barrier()


B_, SEQ_ = 4, 8192


def _run(build_fn, in_maps):
    nc = bass.Bass("TRN2", target_bir_lowering=False)
    with ExitStack() as st:
        k = K(nc, st)
        build_fn(k, nc)
        k.S.barrier()
    res = run_bass_kernel_spmd(nc, in_maps, core_ids=list(range(8)))
    return res.results


def kernel(**inp):
    inp = {k_: np.asarray(v) for k_, v in inp.items()}
    c = np.ascontiguousarray
    xT = [c(np.asarray(inp['x'][b], np.float32).T) for b in range(B_)]
    parts = None
    for i in range(4):
        maps = []
        for b in range(B_):
            for hh in range(2):
                d = prep_mla(inp, i, b, hh, SEQ_) if i % 2 == 0 else prep_rwkv(inp, i, b, hh)
                d['xT'] = xT[b]
                if parts is not None:
                    d['p0T'] = parts[b][0]; d['p1T'] = parts[b][1]
                maps.append(d)
        if i % 2 == 0:
            hp = parts is not None
            res = _run(lambda k, nc, hp=hp: mla_layer(k, nc, SEQ_, hp), maps)
        else:
            res = _run(lambda k, nc: rwkv_layer(k, nc, SEQ_), maps)
        if parts is not None:
            xT = [res[2 * b]['xnT'] for b in range(B_)]
        parts = [(res[2 * b]['pT'], res[2 * b + 1]['pT']) for b in range(B_)]
    maps = []
    half = SEQ_ // 2
    g = pm(inp['final_g'], 8)
    for b in range(B_):
        for hh in range(2):
            sl = slice(hh * half, (hh + 1) * half)
            maps.append({'xT': c(xT[b][:, sl]), 'p0T': c(parts[b][0][:, sl]), 'p1T': c(parts[b][1][:, sl]), 'gains': g})
    res = _run(lambda k, nc: final_layer(k, nc, half), maps)
    out = np.empty((B_, SEQ_, 1024), np.float32)
    for b in range(B_):
        for hh in range(2):
            out[b, hh * half:(hh + 1) * half, :] = res[2 * b + hh]['oT'].T
    return out
```
